# Optimizing a Trainium2 kernel written in Bass

```python
import math
import jax
import jax.numpy as jnp
from jax import lax
import numpy as np

D_MODEL = 2048
BATCH = 32
SEQ = 256
DEPTH = 4
DEC_BATCH = 4
DEC_SEQ = 1024
PAST_LEN = 256

GRID_W = 64
N_HEADS = 8
N_KV_HEADS = 2
HEAD_DIM = 128
AXIS_DIM = HEAD_DIM // 2
ROPE_THETA = 10000.0
Q_BLOCK = 128
D_ATTN = N_HEADS * HEAD_DIM
D_KV = N_KV_HEADS * HEAD_DIM
D_RNN = D_MODEL // 2
N_RNN_BLOCKS = 8
RNN_BLOCK = D_RNN // N_RNN_BLOCKS
RNN_CONV_W = 4
LRU_C = 8.0
D_SSM = D_MODEL // 2
SSM_GROUP = 16
N_SSM_GROUPS = D_SSM // SSM_GROUP
SSM_STATE = 64
D_FF = 2 * D_MODEL
FFN_CONV_W = 3
N_BRANCH = 3
N_MOD = 6
EPS = 1e-6
SPLIT_SIZES = (D_ATTN, D_KV, D_KV, D_RNN, D_RNN, D_SSM, N_BRANCH * D_MODEL)
D_IN = D_ATTN + 2 * D_KV + 2 * D_RNN + D_SSM + N_BRANCH * D_MODEL

kernel_name = 'hybrid_dit_attn_rglru_s5_step'

F32 = jnp.float32


def rms_norm(x, g):
    xf = x.astype(F32)
    y = xf * lax.rsqrt(jnp.mean(xf * xf, axis=-1, keepdims=True) + EPS)
    return (y * g.astype(F32)).astype(x.dtype)


def dwconv_centred(x, w, b):
    width = w.shape[0]
    left = width // 2
    y = lax.conv_general_dilated(
        x, w[:, None, :].astype(x.dtype), window_strides=(1,),
        padding=[(left, width - 1 - left)],
        dimension_numbers=('NWC', 'WIO', 'NWC'),
        feature_group_count=x.shape[-1])
    return y + b.astype(x.dtype)


def linear_scan(a, b, h0, reverse):
    def comb(e1, e2):
        a1, b1 = e1
        a2, b2 = e2
        return a1 * a2, a2 * b1 + b2
    a_cum, b_cum = lax.associative_scan(comb, (a, b), reverse=reverse, axis=1)
    h = a_cum * h0[:, None] + b_cum
    final = h[:, 0] if reverse else h[:, -1]
    return h, final


def complex_linear_scan(ar, ai, br, bi, h0r, h0i, reverse):
    def comb(e1, e2):
        a1r, a1i, b1r, b1i = e1
        a2r, a2i, b2r, b2i = e2
        return (a2r * a1r - a2i * a1i, a2r * a1i + a2i * a1r,
                a2r * b1r - a2i * b1i + b2r, a2r * b1i + a2i * b1r + b2i)
    a_r, a_i, b_r, b_i = lax.associative_scan(comb, (ar, ai, br, bi), reverse=reverse, axis=1)
    h0r = h0r[:, None]
    h0i = h0i[:, None]
    hr = a_r * h0r - a_i * h0i + b_r
    hi = a_r * h0i + a_i * h0r + b_i
    if reverse:
        return hr, hi, hr[:, 0], hi[:, 0]
    return hr, hi, hr[:, -1], hi[:, -1]


def axial_rope(n_tokens):
    rows = n_tokens // GRID_W
    t_row = jnp.repeat(jnp.arange(rows, dtype=F32), GRID_W)
    t_col = (jnp.arange(n_tokens) % GRID_W).astype(F32)
    inv = 1.0 / (ROPE_THETA ** (jnp.arange(0, AXIS_DIM, 2, dtype=F32) / AXIS_DIM))
    ang = jnp.stack([t_row[:, None] * inv, t_col[:, None] * inv], axis=0)
    return jnp.cos(ang), jnp.sin(ang)


def apply_rope(x, cos, sin):
    xf = x.astype(F32)
    half = AXIS_DIM // 2
    parts = []
    for ax in range(2):
        seg = xf[..., ax * AXIS_DIM:(ax + 1) * AXIS_DIM]
        x1, x2 = seg[..., :half], seg[..., half:]
        cs = cos[ax][:, None, :]
        sn = sin[ax][:, None, :]
        parts.append(x1 * cs - x2 * sn)
        parts.append(x2 * cs + x1 * sn)
    return jnp.concatenate(parts, axis=-1).astype(x.dtype)


def block_attention(q, k, v):
    n, sq, _, dh = q.shape
    n_blk = sq // Q_BLOCK
    grp = N_HEADS // N_KV_HEADS
    qb = q.reshape(n, n_blk, Q_BLOCK, N_KV_HEADS, grp, dh).transpose(1, 0, 2, 3, 4, 5)
    scale = dh ** -0.5

    def one_block(q_blk):
        s = jnp.einsum('bqkgd,bskd->bkgqs', q_blk, k, preferred_element_type=F32) * scale
        p = jax.nn.softmax(s, axis=-1).astype(v.dtype)
        return jnp.einsum('bkgqs,bskd->bqkgd', p, v)

    o = lax.map(one_block, qb)
    return o.transpose(1, 0, 2, 3, 4, 5).reshape(n, sq, N_HEADS * dh)


def rg_lru_bidir(x, h0, gate_w, gate_b, lam):
    n, L, _ = x.shape
    xf = x.astype(F32)
    xb = xf.reshape(n, L, N_RNN_BLOCKS, RNN_BLOCK)
    hs, finals = [], []
    for d in range(2):
        gates = jnp.einsum('blkc,jkcd->jblkd', xb, gate_w[d].astype(F32)).reshape(2, n, L, D_RNN)
        gates = gates + gate_b[d].astype(F32)[:, None, None, :]
        r = jax.nn.sigmoid(gates[0])
        i = jax.nn.sigmoid(gates[1])
        log_a = -LRU_C * r * jax.nn.softplus(-lam[d].astype(F32))
        a = jnp.exp(log_a)
        b = jnp.sqrt(-jnp.expm1(2.0 * log_a)) * (i * xf)
        h, hf = linear_scan(a, b, h0[:, d].astype(F32), reverse=(d == 1))
        hs.append(h)
        finals.append(hf)
    return (hs[0] + hs[1]).astype(x.dtype), jnp.stack(finals, axis=1).astype(x.dtype)


def s5_bidir(u, h0_re, h0_im, a_re, a_im, log_dt, b_re, b_im, c_re, c_im, d_skip):
    n, L, _ = u.shape
    uf = u.astype(F32).reshape(n, L, N_SSM_GROUPS, SSM_GROUP)
    ys, fr, fi = [], [], []
    for d in range(2):
        lam_r = jnp.minimum(a_re[d].astype(F32), -1e-4)
        lam_i = a_im[d].astype(F32)
        dt = jnp.exp(log_dt[d].astype(F32))[:, None]
        mag = jnp.exp(lam_r * dt)
        ab_r = mag * jnp.cos(lam_i * dt)
        ab_i = mag * jnp.sin(lam_i * dt)
        den = lam_r * lam_r + lam_i * lam_i
        q_r = ((ab_r - 1.0) * lam_r + ab_i * lam_i) / den
        q_i = (ab_i * lam_r - (ab_r - 1.0) * lam_i) / den
        br = b_re[d].astype(F32)
        bi = b_im[d].astype(F32)
        bb_r = q_r[..., None] * br - q_i[..., None] * bi
        bb_i = q_r[..., None] * bi + q_i[..., None] * br
        bu_r = jnp.einsum('gph,blgh->blgp', bb_r, uf)
        bu_i = jnp.einsum('gph,blgh->blgp', bb_i, uf)
        hr, hi, f_r, f_i = complex_linear_scan(
            jnp.broadcast_to(ab_r, bu_r.shape), jnp.broadcast_to(ab_i, bu_r.shape),
            bu_r, bu_i, h0_re[:, d].astype(F32), h0_im[:, d].astype(F32), reverse=(d == 1))
        y = (jnp.einsum('ghp,blgp->blgh', c_re[d].astype(F32), hr)
             - jnp.einsum('ghp,blgp->blgh', c_im[d].astype(F32), hi))
        ys.append(y)
        fr.append(f_r)
        fi.append(f_i)
    y = (ys[0] + ys[1]).reshape(n, L, D_SSM) + d_skip.astype(F32) * u.astype(F32)
    return (y.astype(u.dtype), jnp.stack(fr, axis=1).astype(u.dtype),
            jnp.stack(fi, axis=1).astype(u.dtype))


def trunk_layer(x, mod, p, rope, ctx):
    n, L, _ = x.shape
    shift1, scale1, gate1, shift2, scale2, gate2 = [mod[:, :, i] for i in range(N_MOD)]
    h = rms_norm(x, p['g_mix']) * (1 + scale1) + shift1
    z = h @ p['w_in']
    offs = np.cumsum(SPLIT_SIZES)[:-1].tolist()
    q, k, v, xr, xg, us, gl = jnp.split(z, offs, axis=-1)
    q = rms_norm(q.reshape(n, L, N_HEADS, HEAD_DIM), p['q_gain'])
    k = rms_norm(k.reshape(n, L, N_KV_HEADS, HEAD_DIM), p['k_gain'])
    v = v.reshape(n, L, N_KV_HEADS, HEAD_DIM)
    if ctx is None:
        k_all, v_all = k, v
        h0_lru = jnp.zeros((n, 2, D_RNN), x.dtype)
        h0_sr = jnp.zeros((n, 2, N_SSM_GROUPS, SSM_STATE), x.dtype)
        h0_si = h0_sr
    else:
        ck, cv, h0_lru, h0_sr, h0_si = ctx
        q = apply_rope(q, rope[0], rope[1])
        k = apply_rope(k, rope[0], rope[1])
        k_all = jnp.concatenate([ck.astype(k.dtype), k], axis=1)
        v_all = jnp.concatenate([cv.astype(v.dtype), v], axis=1)
    attn = block_attention(q, k_all, v_all)
    xr = dwconv_centred(xr, p['rnn_conv_w'], p['rnn_conv_b'])
    lru, lru_f = rg_lru_bidir(xr, h0_lru, p['lru_gate_w'], p['lru_gate_b'], p['lru_lambda'])
    rec = jax.nn.gelu(xg) * lru
    ssm, s5_fr, s5_fi = s5_bidir(us, h0_sr, h0_si, p['s5_a_re'], p['s5_a_im'], p['s5_log_dt'],
                                 p['s5_b_re'], p['s5_b_im'], p['s5_c_re'], p['s5_c_im'], p['s5_d'])
    ssm = jax.nn.gelu(ssm)
    ssm = ssm * jax.nn.sigmoid(ssm @ p['glu_w'] + p['glu_b'])
    g = jax.nn.sigmoid(gl).reshape(n, L, N_BRANCH, D_MODEL)
    merged = (g[:, :, 0] * (attn @ p['w_pa']) + g[:, :, 1] * (rec @ p['w_pr'])
              + g[:, :, 2] * (ssm @ p['w_ps']))
    x = x + gate1 * (merged @ p['w_o'])
    h2 = rms_norm(x, p['g_ffn']) * (1 + scale2) + shift2
    a, b = jnp.split(h2 @ p['w_up'], 2, axis=-1)
    a = dwconv_centred(a, p['ffn_conv_w'], p['ffn_conv_b'])
    x = x + gate2 * ((jax.nn.gelu(a) * b) @ p['w_down'])
    return x, (k, v, lru_f, s5_fr, s5_fi)


def setup_inputs(seed: int = 0) -> dict:
    key = jax.random.key(seed)
    ks = iter(jax.random.split(key, 64))

    def nrm(shape, s):
        return jax.random.normal(next(ks), shape, F32) * s

    G, P, H = N_SSM_GROUPS, SSM_STATE, SSM_GROUP
    lam_u = jax.random.uniform(next(ks), (DEPTH, 2, D_RNN), F32, 0.9, 0.999)
    lru_lambda = jnp.log(lam_u) - jnp.log1p(-lam_u)
    s5_a_im = math.pi * jnp.arange(P, dtype=F32) + nrm((DEPTH, 2, G, P), 0.01)
    s5_log_dt = jax.random.uniform(next(ks), (DEPTH, 2, G), F32, math.log(1e-3), math.log(1e-1))
    return {
        'x_prompt': nrm((BATCH, SEQ, D_MODEL), 1.0),
        'x_sample': nrm((DEC_BATCH, DEC_SEQ, D_MODEL), 1.0),
        'c': nrm((DEC_BATCH, D_MODEL), 1.0),
        'cache_k': nrm((DEC_BATCH, DEPTH, PAST_LEN, N_KV_HEADS, HEAD_DIM), 1.0),
        'cache_v': nrm((DEC_BATCH, DEPTH, PAST_LEN, N_KV_HEADS, HEAD_DIM), 1.0),
        'state_lru': nrm((DEC_BATCH, DEPTH, 2, D_RNN), 0.5),
        'state_s5_re': nrm((DEC_BATCH, DEPTH, 2, G, P), 0.5),
        'state_s5_im': nrm((DEC_BATCH, DEPTH, 2, G, P), 0.5),
        'c_ctx': nrm((D_MODEL,), 1.0),
        'w_mod': nrm((DEPTH, D_MODEL, N_MOD * D_MODEL), D_MODEL ** -0.5),
        'b_mod': nrm((DEPTH, N_MOD * D_MODEL), 0.02),
        'g_mix': 1.0 + nrm((DEPTH, D_MODEL), 0.02),
        'w_in': nrm((DEPTH, D_MODEL, D_IN), D_MODEL ** -0.5),
        'q_gain': 1.0 + nrm((DEPTH, HEAD_DIM), 0.02),
        'k_gain': 1.0 + nrm((DEPTH, HEAD_DIM), 0.02),
        'rnn_conv_w': nrm((DEPTH, RNN_CONV_W, D_RNN), RNN_CONV_W ** -0.5),
        'rnn_conv_b': nrm((DEPTH, D_RNN), 0.01),
        'lru_gate_w': nrm((DEPTH, 2, 2, N_RNN_BLOCKS, RNN_BLOCK, RNN_BLOCK), RNN_BLOCK ** -0.5),
        'lru_gate_b': nrm((DEPTH, 2, 2, D_RNN), 0.1),
        'lru_lambda': lru_lambda,
        's5_a_re': -0.5 + nrm((DEPTH, 2, G, P), 0.01),
        's5_a_im': s5_a_im,
        's5_log_dt': s5_log_dt,
        's5_b_re': nrm((DEPTH, 2, G, P, H), (2 * H) ** -0.5),
        's5_b_im': nrm((DEPTH, 2, G, P, H), (2 * H) ** -0.5),
        's5_c_re': nrm((DEPTH, 2, G, H, P), (2 * P) ** -0.5),
        's5_c_im': nrm((DEPTH, 2, G, H, P), (2 * P) ** -0.5),
        's5_d': nrm((DEPTH, D_SSM), 1.0),
        'glu_w': nrm((DEPTH, D_SSM, D_SSM), D_SSM ** -0.5),
        'glu_b': nrm((DEPTH, D_SSM), 0.02),
        'w_pa': nrm((DEPTH, D_ATTN, D_MODEL), D_ATTN ** -0.5),
        'w_pr': nrm((DEPTH, D_RNN, D_MODEL), D_RNN ** -0.5),
        'w_ps': nrm((DEPTH, D_SSM, D_MODEL), D_SSM ** -0.5),
        'w_o': nrm((DEPTH, D_MODEL, D_MODEL), D_MODEL ** -0.5),
        'g_ffn': 1.0 + nrm((DEPTH, D_MODEL), 0.02),
        'w_up': nrm((DEPTH, D_MODEL, 2 * D_FF), D_MODEL ** -0.5),
        'ffn_conv_w': nrm((DEPTH, FFN_CONV_W, D_FF), FFN_CONV_W ** -0.5),
        'ffn_conv_b': nrm((DEPTH, D_FF), 0.01),
        'w_down': nrm((DEPTH, D_FF, D_MODEL), D_FF ** -0.5),
    }


def reference(x_prompt, x_sample, c, cache_k, cache_v, state_lru, state_s5_re, state_s5_im,
              c_ctx, w_mod, b_mod, g_mix, w_in, q_gain, k_gain, rnn_conv_w, rnn_conv_b,
              lru_gate_w, lru_gate_b, lru_lambda, s5_a_re, s5_a_im, s5_log_dt, s5_b_re, s5_b_im,
              s5_c_re, s5_c_im, s5_d, glu_w, glu_b, w_pa, w_pr, w_ps, w_o, g_ffn, w_up,
              ffn_conv_w, ffn_conv_b, w_down):
    rope = axial_rope(x_sample.shape[1])
    yp = x_prompt
    ys = x_sample
    new_k, new_v, new_lru, new_sr, new_si = [], [], [], [], []
    for l in range(DEPTH):
        p = {
            'g_mix': g_mix[l], 'w_in': w_in[l], 'q_gain': q_gain[l], 'k_gain': k_gain[l],
            'rnn_conv_w': rnn_conv_w[l], 'rnn_conv_b': rnn_conv_b[l],
            'lru_gate_w': lru_gate_w[l], 'lru_gate_b': lru_gate_b[l], 'lru_lambda': lru_lambda[l],
            's5_a_re': s5_a_re[l], 's5_a_im': s5_a_im[l], 's5_log_dt': s5_log_dt[l],
            's5_b_re': s5_b_re[l], 's5_b_im': s5_b_im[l], 's5_c_re': s5_c_re[l],
            's5_c_im': s5_c_im[l], 's5_d': s5_d[l], 'glu_w': glu_w[l], 'glu_b': glu_b[l],
            'w_pa': w_pa[l], 'w_pr': w_pr[l], 'w_ps': w_ps[l], 'w_o': w_o[l],
            'g_ffn': g_ffn[l], 'w_up': w_up[l], 'ffn_conv_w': ffn_conv_w[l],
            'ffn_conv_b': ffn_conv_b[l], 'w_down': w_down[l],
        }
        mod_ctx = (jax.nn.silu(c_ctx)[None] @ w_mod[l] + b_mod[l]).reshape(1, 1, N_MOD, D_MODEL)
        mod_lat = (jax.nn.silu(c) @ w_mod[l] + b_mod[l]).reshape(-1, 1, N_MOD, D_MODEL)
        yp, st = trunk_layer(yp, mod_ctx, p, None, None)
        new_k.append(st[0])
        new_v.append(st[1])
        new_lru.append(st[2])
        new_sr.append(st[3])
        new_si.append(st[4])
        ctx = (cache_k[:, l], cache_v[:, l], state_lru[:, l], state_s5_re[:, l], state_s5_im[:, l])
        ys, _ = trunk_layer(ys, mod_lat, p, rope, ctx)
    new_cache_k = jnp.stack(new_k, axis=1)
    new_cache_v = jnp.stack(new_v, axis=1)
    new_state_lru = jnp.stack(new_lru, axis=1)
    new_state_s5_re = jnp.stack(new_sr, axis=1)
    new_state_s5_im = jnp.stack(new_si, axis=1)
    return (yp, ys, new_cache_k, new_cache_v, new_state_lru, new_state_s5_re, new_state_s5_im)
```

```python
import math
import numpy as np
import concourse.bass as bass
import concourse.mybir as mybir
from concourse.bass_utils import run_bass_kernel_spmd

F32 = mybir.dt.float32
BF16 = mybir.dt.bfloat16
I32 = mybir.dt.int32
AF = mybir.ActivationFunctionType
ALU = mybir.AluOpType

DEPTH = 4
D = 2048
T = 1024
DIN = 10752
NPV = 362
PV_GMIX, PV_GFFN, PV_QG, PV_KG, PV_RCW, PV_RCB, PV_LGB, PV_LAM, PV_S5D, PV_GLUB, PV_FCW, PV_FCB, PV_BMOD = (
    0, 16, 32, 33, 34, 66, 74, 106, 122, 130, 138, 234, 266)
TWO_PI = 2.0 * math.pi
ESZ = {F32: 4, BF16: 2, I32: 4}
NDMA = {"sp": 8, "pool": 4}


class Prog:
    def __init__(self):
        self.q = {e: [] for e in ("pe", "act", "dve", "pool", "sp")}
        self.cnt = {e: 0 for e in self.q}
        self.dman = {"sp": 0, "pool": 0}
        self.lastw = {}
        self.rd = {}
        self.seen = {e: {} for e in self.q}

    @staticmethod
    def keys(aps):
        ks = []
        for a in aps:
            if a is None or isinstance(a, (int, float)):
                continue
            nm = a.tensor.name
            sp = str(a.space)
            if "DRAM" in sp.upper() or "HBM" in sp.upper():
                if nm.startswith("scr_"):
                    ks.append((nm, 0))
                continue
            es = ESZ.get(a.dtype, 4)
            pat = a.ap
            row = pat[0][0]
            lo = a.offset % row if row > 0 else a.offset
            hi = lo
            for st, n in pat[1:]:
                if st >= 0:
                    hi += st * (n - 1)
                else:
                    lo += st * (n - 1)
            b0 = (lo * es) // 2048
            b1 = (hi * es + es - 1) // 2048
            for b in range(b0, b1 + 1):
                ks.append((nm, b))
        return ks

    def add(self, eng, fn, ins, outs, dma=False):
        rk = self.keys(ins)
        wk = self.keys(outs)
        deps = {}

        def need(tok):
            if tok is not None:
                if deps.get(tok[0], 0) < tok[1]:
                    deps[tok[0]] = tok[1]

        for k in rk:
            need(self.lastw.get(k))
        for k in wk:
            need(self.lastw.get(k))
            for sk, v in self.rd.get(k, {}).items():
                need((sk, v))
        if dma:
            j = self.dman[eng]
            n = NDMA[eng]
            sk = (eng, j % n)
            val = 16 * (j // n + 1)
            self.dman[eng] = j + 1
            if val > 16:
                need((sk, val - 16))
            tok = (sk, val)
        else:
            self.cnt[eng] += 1
            tok = ((eng, "c"), self.cnt[eng])
        if eng == "pe":
            deps.pop(("pe", "c"), None)
        waits = []
        seen = self.seen[eng]
        for sk, v in deps.items():
            if seen.get(sk, 0) < v:
                waits.append((sk, v))
                seen[sk] = v
        self.q[eng].append((fn, waits, tok))
        for k in wk:
            self.lastw[k] = tok
            self.rd[k] = {}
        for k in rk:
            d = self.rd.setdefault(k, {})
            if d.get(tok[0], 0) < tok[1]:
                d[tok[0]] = tok[1]


class _Stop(Exception):
    pass


def build_program(depth=DEPTH, dbg=False, passes=("P", "S"), lmap=None, stop_at=None):
    nc = bass.Bass("TRN2", target_bir_lowering=False)
    P = Prog()

    def din(name, shape, dt=F32):
        return nc.dram_tensor(name, list(shape), dt, kind="ExternalInput").ap()

    def dout(name, shape, dt=F32):
        return nc.dram_tensor(name, list(shape), dt, kind="ExternalOutput").ap()

    xin = {"P": din("xp", [128, 16, T]), "S": din("xs", [128, 16, T])}
    cv = din("cv", [128, 16, 2])
    wmod = din("wmod", [DEPTH, D, 6 * D])
    win = din("win", [DEPTH, D, DIN])
    wpa = din("wpa", [DEPTH, 1024, D])
    wpr = din("wpr", [DEPTH, 1024, D])
    wps = din("wps", [DEPTH, 1024, D])
    wo = din("wo", [DEPTH, D, D])
    wup = din("wup", [DEPTH, D, 4 * D])
    wdn = din("wdn", [DEPTH, 2 * D, D])
    glw = din("glw", [DEPTH, 1024, 1024])
    lgw = din("lgw", [DEPTH, 128, 32 * 128])
    pvec = din("pvec", [128, DEPTH, NPV])
    s5sc = din("s5sc", [DEPTH, 128, 3, 64])
    wbd = din("wbd", [DEPTH, 4, 128, 4096])
    cwc = din("cwc", [DEPTH, 128, 4, 1024])
    h0s5 = din("h0s5", [DEPTH, 128, 2, 64])
    h0lru = din("h0lru", [DEPTH, 128, 16])
    ckT = din("ckT", [DEPTH, 128, 2, 256])
    cvv = din("cvv", [DEPTH, 128, 2, 256])
    cosd = din("cosd", [128, T])
    sind = din("sind", [128, T])
    protd = din("protd", [128, 128])
    identd = din("identd", [128, 128])
    iotad = din("iotad", [128, T])
    yout = {"P": dout("yp", [128, 16, T]), "S": dout("ys", [128, 16, T])}
    nk = dout("nk", [DEPTH, T, 256])
    nv = dout("nv", [DEPTH, T, 256])
    nlru = dout("nlru", [DEPTH, 64, 128])
    ns5 = [dout("ns5r", [DEPTH, 256, 128]), dout("ns5i", [DEPTH, 256, 128])]
    xsp = nc.dram_tensor("scr_x", [128, 16, T], F32, kind="ExternalOutput").ap()

    At = nc.alloc_sbuf_tensor("A", [128, 16384], F32)
    Ht = nc.alloc_sbuf_tensor("H", [128, 16, T], BF16)
    Et = nc.alloc_sbuf_tensor("E", [128, 40 * T], BF16)
    WRt = [nc.alloc_sbuf_tensor("WR%d" % i, [128, 4096], BF16) for i in range(2)]
    IDt = nc.alloc_sbuf_tensor("ID", [128, 128], F32)
    ONt = nc.alloc_sbuf_tensor("ON", [128, 128], BF16)
    PVt = nc.alloc_sbuf_tensor("PV", [128, NPV], F32)
    MODt = nc.alloc_sbuf_tensor("MOD", [128, DEPTH, 2, 96], F32)
    CVt = nc.alloc_sbuf_tensor("CV", [128, 16, 2], F32)
    SCVt = nc.alloc_sbuf_tensor("SCV", [128, 16, 2], BF16)
    LPt = nc.alloc_sbuf_tensor("LP", [128, 128], F32)
    S5t = nc.alloc_sbuf_tensor("S5S", [128, 15, 64], F32)
    S5i = nc.alloc_sbuf_tensor("S5I", [128, 64], I32)
    FINt = nc.alloc_sbuf_tensor("FIN", [128, 64], F32)
    FSt = nc.alloc_sbuf_tensor("FS5", [128, 2, 256], F32)
    H0t = nc.alloc_sbuf_tensor("H0", [128, 2, 64], F32)
    H0Lt = nc.alloc_sbuf_tensor("H0L", [128, 16], F32)
    SMt = nc.alloc_sbuf_tensor("SM", [128, 512], F32)
    PSt = nc.alloc_psum_tensor("PS", [128, 4096], F32)

    A32 = At
    A16 = At.bitcast(BF16)
    AI32 = At.bitcast(I32)
    E16 = Et[:, :].rearrange("p (c t) -> p c t", c=40)
    E32 = Et.bitcast(F32)
    ID = IDt[:, :]
    ONES = ONt[:, :]

    def ps(b, n=512, p=128):
        return PSt[0:p, b * 512:b * 512 + n]

    def dma(eng, out, in_, **kw):
        P.add(eng, lambda e, o=out, i=in_, k=kw: e.dma_start(out=o, in_=i, **k), [in_], [out], dma=True)

    def mm(out, lhsT, rhs, start=True, stop=True):
        P.add("pe", lambda e: e.matmul(out, lhsT, rhs, start=start, stop=stop), [lhsT, rhs], [out])

    def act(out, in_, func, scale=1.0, bias=0.0):
        ins = [in_] + [x for x in (scale, bias) if not isinstance(x, (int, float))]
        P.add("act", lambda e: e.activation(out=out, in_=in_, func=func, scale=scale, bias=bias), ins, [out])

    def tt(out, in0, in1, op, eng="dve"):
        P.add(eng, lambda e: e.tensor_tensor(out=out, in0=in0, in1=in1, op=op), [in0, in1], [out])

    def ts(out, in0, s1, s2, op0, op1=None, eng="dve"):
        ins = [in0] + [x for x in (s1, s2) if x is not None and not isinstance(x, (int, float))]
        if op1 is None:
            P.add(eng, lambda e: e.tensor_scalar(out=out, in0=in0, scalar1=s1, scalar2=None, op0=op0), ins, [out])
        else:
            P.add(eng, lambda e: e.tensor_scalar(out=out, in0=in0, scalar1=s1, scalar2=s2, op0=op0, op1=op1), ins, [out])

    def stt(out, in0, scalar, in1, op0, op1, eng="dve"):
        ins = [in0, in1] + ([] if isinstance(scalar, (int, float)) else [scalar])
        P.add(eng, lambda e: e.scalar_tensor_tensor(out=out, in0=in0, scalar=scalar, in1=in1, op0=op0, op1=op1), ins, [out])

    def cp(out, in_, eng="dve"):
        P.add(eng, lambda e: e.tensor_copy(out=out, in_=in_), [in_], [out])

    def recip(out, in_):
        P.add("dve", lambda e: e.reciprocal(out=out, in_=in_), [in_], [out])

    def scan(out, d0, d1, init):
        ins = [d0, d1] + ([] if isinstance(init, (int, float)) else [init])
        P.add("dve", lambda e: e.tensor_tensor_scan(out=out, data0=d0, data1=d1, initial=init, op0=ALU.mult, op1=ALU.add), ins, [out])

    def memset(out, v, eng="dve"):
        P.add(eng, lambda e: e.memset(out, v), [], [out])

    wstate = {"i": 0}

    def wload(src, KC, ncol):
        slot = wstate["i"] % 2
        wstate["i"] += 1
        wt = WRt[slot][:, 0:KC * ncol].rearrange("p (k n) -> p k n", k=KC)
        dma("pool", wt, src)
        return wt

    bankrot = {"i": 0}

    def nextbank(banks):
        b = banks[bankrot["i"] % len(banks)]
        bankrot["i"] += 1
        return b

    def proj(w2d, KC, c0, ncols, rhs_fn, consumer, banks=(0, 1, 2, 3), n0=0):
        ct = 4096 // KC
        ct = min(ct, ncols)
        wv = w2d.rearrange("(k p) n -> p k n", p=128)
        for t in range(ncols // ct):
            wt = wload(wv[:, :, c0 + t * ct:c0 + (t + 1) * ct], KC, ct)
            for cc in range(ct // 128):
                n = n0 + t * (ct // 128) + cc
                for hf in range(2):
                    b = nextbank(banks)
                    for k in range(KC):
                        mm(ps(b), wt[:, k, cc * 128:(cc + 1) * 128], rhs_fn(k, hf), start=(k == 0), stop=(k == KC - 1))
                    consumer(n, hf, b)

    def Hr(k, hf):
        return Ht[:, k, hf * 512:(hf + 1) * 512]

    def a32(slot, n=1024, off=0):
        return A32[:, slot * 1024 + off: slot * 1024 + off + n]

    def a16(slot, n=1024, off=0):
        return A16[:, slot * 2048 + off: slot * 2048 + off + n]

    def e32(chunk, n=1024, off=0):
        return E32[:, chunk * 512 + off: chunk * 512 + off + n]

    def XA(c, hf=None):
        if hf is None:
            return A32[:, c * 1024:(c + 1) * 1024]
        return A32[:, c * 1024 + hf * 512: c * 1024 + (hf + 1) * 512]

    dma("sp", ID, identd[:, :])
    PVA = A32[:, 0:DEPTH * NPV].rearrange("p (l n) -> p l n", l=DEPTH)
    dma("sp", PVA, pvec[:, :, :])
    ROWt = SMt
    dma("sp", CVt[:, :, :], cv[:, :, :])
    memset(ONES, 1.0)
    act(SCVt[:, :, :], CVt[:, :, :], AF.Silu)

    for l in range(depth):
        wv = wmod[l].rearrange("(k p) n -> p k n", p=128)
        for ng in range(48):
            wt = wload(wv[:, :, ng * 256:(ng + 1) * 256], 16, 256)
            b = nextbank((0, 1))
            for k in range(16):
                mm(ps(b, 256, 2), SCVt[:, k, :], wt[:, k, :], start=(k == 0), stop=(k == 15))
            act(ROWt[0:2, 0:256], ps(b, 256, 2), AF.Copy)
            for cc in range(2):
                b2 = nextbank((2, 3))
                mm(ps(b2, 2), ROWt[0:2, cc * 128:(cc + 1) * 128], IDt[0:2, 0:2])
                ch = ng * 2 + cc
                tt(MODt[:, l, :, ch], ps(b2, 2), PVA[:, l, PV_BMOD + ch:PV_BMOD + ch + 1].to_broadcast([128, 2]), ALU.add)

    def run_pass(ps_name):
        isS = ps_name == "S"
        cond = 1 if isS else 0
        nseq, L = (1, T) if isS else (4, 256)

        def sv(ap):
            return ap.rearrange("p (s l) -> p s l", s=nseq)

        for c in range(16):
            dma("sp", XA(c), xin[ps_name][:, c, :])

        for l in range(depth):
            PV = lambda off, n=1: PVt[:, off:off + n]
            dma("sp", PVt[:, :], pvec[:, l, :])
            MOD = lambda i, c: MODt[:, l, cond, i * 16 + c: i * 16 + c + 1]

            if stop_at == (l, 0):
                raise _Stop()
            for i, (pvo, mi) in enumerate(((PV_GMIX, 1), (PV_GFFN, 4))):
                ts(LPt[:, 32:48], MODt[:, l, cond, mi * 16:(mi + 1) * 16], 1.0, None, ALU.add)
                tt(LPt[:, i * 16:(i + 1) * 16], LPt[:, 32:48], PV(pvo, 16), ALU.mult)
            act(LPt[:, 80:96], PV(PV_LAM, 16), AF.Exp, scale=-1.0)
            act(LPt[:, 80:96], LPt[:, 80:96], AF.Ln, bias=1.0)
            ts(LPt[:, 48:64], LPt[:, 80:96], -8.0, None, ALU.mult)
            ts(LPt[:, 64:80], LPt[:, 80:96], -16.0, None, ALU.mult)

            def norm_to_H(gi, shift_i):
                for c in range(16):
                    sq = E16[:, 32 + (c % 2), :]
                    act(sq, XA(c), AF.Square)
                    for hf in range(2):
                        mm(ps(4 + hf), ONES, sq[:, hf * 512:(hf + 1) * 512], start=(c == 0), stop=(c == 15))
                rstd = e32(34)
                for hf in range(2):
                    act(rstd[:, hf * 512:(hf + 1) * 512], ps(4 + hf), AF.Sqrt, scale=1.0 / D, bias=1e-6)
                recip(rstd, rstd)
                for c in range(16):
                    tmp = e32(36 + 2 * (c % 2))
                    stt(tmp, XA(c), LPt[:, gi * 16 + c: gi * 16 + c + 1], rstd, ALU.mult, ALU.mult)
                    act(Ht[:, c, :], tmp, AF.Identity, scale=1.0, bias=MOD(shift_i, c))

            norm_to_H(0, 0)

            if stop_at == (l, 2):
                raise _Stop()
            COS, SIN = a32(0), a32(1)
            KT = A16[:, 2 * 2048: 2 * 2048 + 2 * 1280].rearrange("p (j t) -> p j t", j=2)
            VT = A16[:, 4 * 2048: 4 * 2048 + 10 * 256].rearrange("p (b n) -> p b n", b=10)
            koff = 256 if isS else 0
            vboff = 2 if isS else 0
            PROT = a32(6, 128, 512)
            if isS:
                dma("sp", PROT, protd[:, :])
                dma("sp", COS, cosd[:, :])
                dma("sp", SIN, sind[:, :])
                dma("pool", KT[:, :, 0:256], ckT[l])
                dma("pool", VT[:, 0:2, :], cvv[l])

            def qknorm(zps, gain_col, out32):
                sq = a16(6, 512)
                act(sq, zps, AF.Square)
                mm(ps(6), ONES, sq)
                rs = a32(7, 512)
                act(rs, ps(6), AF.Sqrt, scale=1.0 / 128, bias=1e-6)
                recip(rs, rs)
                stt(out32, zps, gain_col, rs, ALU.mult, ALU.mult)

            def rope(x32, hf, out16):
                mm(ps(6), PROT, x32)
                t1 = a32(7, 512)
                t2 = a32(7, 512, 512)
                tt(t1, x32, COS[:, hf * 512:(hf + 1) * 512], ALU.mult)
                tt(t2, ps(6), SIN[:, hf * 512:(hf + 1) * 512], ALU.mult)
                tt(out16, t1, t2, ALU.add)

            def k_cons(n, hf, b):
                j = n
                kn = a32(8, 512)
                qknorm(ps(b), PV(PV_KG), kn)
                dst = KT[:, j, koff + hf * 512: koff + (hf + 1) * 512]
                if isS:
                    rope(kn, hf, dst)
                else:
                    cp(dst, kn)
                    for blk in range(4):
                        mm(ps(7, 128), kn[:, blk * 128:(blk + 1) * 128], ID)
                        o = a32(9, 128, (blk % 2) * 128)
                        act(o, ps(7, 128), AF.Copy)
                        t0 = hf * 512 + blk * 128
                        dma("sp", nk[l, t0:t0 + 128, j * 128:(j + 1) * 128], o)

            def v_cons(n, hf, b):
                j = n
                vf = a32(8, 512)
                act(vf, ps(b), AF.Copy)
                for blk in range(4):
                    mm(ps(7, 128), vf[:, blk * 128:(blk + 1) * 128], ID)
                    gb = vboff + hf * 4 + blk
                    if isS:
                        cp(VT[:, gb, j * 128:(j + 1) * 128], ps(7, 128))
                    else:
                        o = a32(9, 128, 512 + (blk % 2) * 128)
                        act(o, ps(7, 128), AF.Copy)
                        cp(VT[:, gb, j * 128:(j + 1) * 128], o)
                        t0 = hf * 512 + blk * 128
                        dma("sp", nv[l, t0:t0 + 128, j * 128:(j + 1) * 128], o)

            if stop_at == (l, 2.05):
                raise _Stop()
            proj(win[l], 16, 1024, 256, Hr, k_cons, banks=(0, 1))
            if stop_at == (l, 2.1):
                raise _Stop()
            proj(win[l], 16, 1280, 256, Hr, v_cons, banks=(0, 1))
            if stop_at == (l, 2.2):
                raise _Stop()

            scl = 1.0 / math.sqrt(128.0)

            def q_cons(n, hf, b):
                h = n
                j = h // 4
                qn = a32(8, 512)
                qknorm(ps(b), PV(PV_QG), qn)
                qT = a16(9, 512, 1536)
                if isS:
                    rope(qn, hf, qT)
                else:
                    cp(qT, qn)
                groups = [(0, 512, list(range(10)))] if isS else [
                    (s * 256, 256, [(hf * 2 + s) * 2, (hf * 2 + s) * 2 + 1]) for s in range(2)]
                pbuf = (h * 2 + hf) % 2
                PT = A16[:, (10 + pbuf * 3) * 2048: (10 + pbuf * 3) * 2048 + 5120].rearrange("p (k q) -> p k q", k=10)
                for (q0, qn_, kbs) in groups:
                    for i, kb in enumerate(kbs):
                        sb = 2 + (i % 2)
                        kcol = kb * 128 if not isS else kb * 128
                        mm(ps(sb, qn_), KT[:, j, kcol:kcol + 128], qT[:, q0:q0 + qn_])
                        act(PT[:, i, q0:q0 + qn_], ps(sb, qn_), AF.Exp, scale=scl)
                    for i, kb in enumerate(kbs):
                        mm(PSt[:, 4 * 512 + q0: 4 * 512 + q0 + qn_], VT[:, kb, j * 128:(j + 1) * 128], PT[:, i, q0:q0 + qn_],
                           start=(i == 0), stop=(i == len(kbs) - 1))
                    for i, kb in enumerate(kbs):
                        mm(PSt[:, 5 * 512 + q0: 5 * 512 + q0 + qn_], ONES, PT[:, i, q0:q0 + qn_],
                           start=(i == 0), stop=(i == len(kbs) - 1))
                rd = a32(7, 512)
                recip(rd, ps(5))
                tt(E16[:, h, hf * 512:(hf + 1) * 512], ps(4), rd, ALU.mult)

            proj(win[l], 16, 0, 1024, Hr, q_cons, banks=(0, 1))

            if stop_at == (l, 3):
                raise _Stop()
            gwt = E16[:, 36:40, :].rearrange("p a b -> p (a b)").rearrange("p (m n) -> p m n", m=32)
            dma("pool", gwt, lgw[l].rearrange("p (m n) -> p m n", m=32))
            if isS:
                dma("sp", H0Lt[:, :], h0lru[l])
            st3 = {}

            def xr_cons(n, hf, b):
                c = n
                XR = a32(0)
                act(XR[:, hf * 512:(hf + 1) * 512], ps(b), AF.Copy)
                if hf == 0:
                    return
                Y, YB = a32(1), a16(2)
                rcw = lambda jj: PV(PV_RCW + jj * 8 + c)
                act(Y, XR, AF.Identity, scale=rcw(2), bias=PV(PV_RCB + c))
                Xs, Ys = sv(XR), sv(Y)
                stt(Ys[:, :, 2:L], Xs[:, :, 0:L - 2], rcw(0), Ys[:, :, 2:L], ALU.mult, ALU.add)
                stt(Ys[:, :, 1:L], Xs[:, :, 0:L - 1], rcw(1), Ys[:, :, 1:L], ALU.mult, ALU.add)
                stt(Ys[:, :, 0:L - 1], Xs[:, :, 1:L], rcw(3), Ys[:, :, 0:L - 1], ALU.mult, ALU.add)
                cp(YB, Y)
                HD = [a32(8), a32(9)]
                for d in range(2):
                    Rg, Ig, Av, Mv, Bv = a32(3), a32(4), a32(5), a32(6), a32(7)
                    for jg, G in enumerate((Rg, Ig)):
                        for h2 in range(2):
                            bb = nextbank((2, 3))
                            mm(ps(bb), gwt[:, (d * 2 + jg) * 8 + c, :], YB[:, h2 * 512:(h2 + 1) * 512])
                            act(G[:, h2 * 512:(h2 + 1) * 512], ps(bb), AF.Sigmoid, bias=PV(PV_LGB + (d * 2 + jg) * 8 + c))
                    act(Av, Rg, AF.Exp, scale=LPt[:, 48 + d * 8 + c: 49 + d * 8 + c])
                    act(Mv, Rg, AF.Exp, scale=LPt[:, 64 + d * 8 + c: 65 + d * 8 + c])
                    act(Mv, Mv, AF.Sqrt, scale=-1.0, bias=1.0)
                    tt(Bv, Mv, Ig, ALU.mult)
                    tt(Bv, Bv, Y, ALU.mult)
                    for s in range(nseq):
                        sl = slice(s * L, (s + 1) * L)
                        init = H0Lt[:, d * 8 + c: d * 8 + c + 1] if isS else 0.0
                        if d == 0:
                            scan(HD[0][:, sl], Av[:, sl], Bv[:, sl], init)
                        else:
                            hb = HD[1][:, sl]
                            scan(hb[:, ::-1], Av[:, sl][:, ::-1], Bv[:, sl][:, ::-1], init)
                    if not isS:
                        fin = FINt[:, :].rearrange("p (s d c) -> p s d c", s=4, d=2)[:, :, d, c]
                        src = sv(HD[d])[:, :, L - 1] if d == 0 else sv(HD[d])[:, :, 0]
                        cp(fin, src)
                tt(HD[0], HD[0], HD[1], ALU.add)
                st3[c] = True

            def xg_cons(n, hf, b):
                c = n
                G = a32(10, 512, hf * 512)
                act(G, ps(b), AF.Gelu_apprx_tanh)
                tt(E16[:, 8 + c, hf * 512:(hf + 1) * 512], G, a32(8)[:, hf * 512:(hf + 1) * 512], ALU.mult)

            for c in range(8):
                proj(win[l], 16, 1536 + c * 128, 128, Hr, xr_cons, banks=(0, 1), n0=c)
                proj(win[l], 16, 2560 + c * 128, 128, Hr, xg_cons, banks=(0, 1), n0=c)
            if not isS:
                mm(ps(7, 128, 64), FINt[:, :], ID)
                o = SMt[0:64, 0:128]
                act(o, ps(7, 128, 64), AF.Copy)
                dma("sp", nlru[l], o)

            if stop_at == (l, 4):
                raise _Stop()
            S = lambda i: S5t[:, i, :]
            dma("sp", S5t[:, 0:3, :], s5sc[l])
            ts(S(0), S(0), -1e-4, None, ALU.min)
            act(S(2), S(2), AF.Exp)
            tt(S(3), S(0), S(2), ALU.mult)
            act(S(3), S(3), AF.Exp)
            tt(S(4), S(1), S(2), ALU.mult)
            ts(S(4), S(4), 1.0 / TWO_PI, None, ALU.mult)
            cp(S5i[:, :], S(4))
            tt(S(4), S(4), S5i[:, :], ALU.subtract)
            stt(S(5), S(4), -1.0, S(4), ALU.mult, ALU.max)
            act(S(6), S(4), AF.Sin, scale=TWO_PI)
            act(S(5), S(5), AF.Sin, scale=-TWO_PI, bias=math.pi / 2)
            tt(S(7), S(3), S(5), ALU.mult)
            tt(S(8), S(3), S(6), ALU.mult)
            tt(S(9), S(0), S(0), ALU.mult)
            tt(S(10), S(1), S(1), ALU.mult)
            tt(S(9), S(9), S(10), ALU.add)
            recip(S(9), S(9))
            ts(S(10), S(7), -1.0, None, ALU.add)
            tt(S(11), S(10), S(0), ALU.mult)
            tt(S(12), S(8), S(1), ALU.mult)
            tt(S(11), S(11), S(12), ALU.add)
            tt(S(11), S(11), S(9), ALU.mult)
            tt(S(12), S(8), S(0), ALU.mult)
            tt(S(13), S(10), S(1), ALU.mult)
            tt(S(12), S(12), S(13), ALU.subtract)
            tt(S(12), S(12), S(9), ALU.mult)
            tt(S(13), S(11), S(11), ALU.mult)
            tt(S(14), S(12), S(12), ALU.mult)
            tt(S(13), S(13), S(14), ALU.add)
            recip(S(13), S(13))
            if isS:
                HR0, HI0 = a32(2, 64), a32(2, 64, 512)
                dma("sp", H0t[:, :, :], h0s5[l])
                tt(HR0, H0t[:, 0, :], S(11), ALU.mult)
                tt(HI0, H0t[:, 1, :], S(12), ALU.mult)
                tt(HR0, HR0, HI0, ALU.add)
                tt(HI0, H0t[:, 1, :], S(11), ALU.mult)
                tt(S(14), H0t[:, 0, :], S(12), ALU.mult)
                tt(HI0, HI0, S(14), ALU.subtract)
                tt(H0t[:, 0, :], HR0, S(13), ALU.mult)
                tt(H0t[:, 1, :], HI0, S(13), ALU.mult)
            CWP = E16[:, 24:40, :].rearrange("p a b -> p (a b)").rearrange("p (d r j m) -> p d r j m", d=2, r=2, j=32)
            memset(E16[:, 24:40, :], 0.0)
            CC = A32[:, 2 * 1024: 6 * 1024].rearrange("p (x j m) -> p x j m", x=4, j=32)
            dma("sp", CC, cwc[l].rearrange("p x (j m) -> p x j m", j=32))
            for d in range(2):
                qr = S(11)[:, d * 32:(d + 1) * 32].unsqueeze(2).to_broadcast([128, 32, 32])
                qi = S(12)[:, d * 32:(d + 1) * 32].unsqueeze(2).to_broadcast([128, 32, 32])
                cr, ci = CC[:, d * 2 + 0], CC[:, d * 2 + 1]
                t1 = A32[:, 6 * 1024:7 * 1024].rearrange("p (j m) -> p j m", j=32)
                t2 = A32[:, 7 * 1024:8 * 1024].rearrange("p (j m) -> p j m", j=32)
                t3 = A32[:, 8 * 1024:9 * 1024].rearrange("p (j m) -> p j m", j=32)
                tt(t1, cr, qr, ALU.mult)
                tt(t2, ci, qi, ALU.mult)
                tt(t1, t1, t2, ALU.subtract)
                tt(t2, cr, qi, ALU.mult)
                tt(t3, ci, qr, ALU.mult)
                tt(t2, t2, t3, ALU.add)
                ts(t2, t2, -1.0, None, ALU.mult)
                for r, tsrc in enumerate((t1, t2)):
                    for qq in range(4):
                        cp(CWP[:, d, r, qq::4, qq * 32:(qq + 1) * 32], tsrc[:, qq::4, :])
            IOTA = e32(16)
            dma("sp", IOTA, iotad[:, :])
            SSMP = A16[:, 12 * 2048: 16 * 2048].rearrange("p (c t) -> p c t", c=8)
            TL = L

            def us_cons(n, hf, b):
                c = n
                US, USB = a32(0), a16(1)
                act(US[:, hf * 512:(hf + 1) * 512], ps(b), AF.Copy)
                if hf == 0:
                    return
                cp(USB, US)
                first = True
                for qq in range(4):
                    j = 4 * c + qq
                    for d in range(2):
                        col = d * 32 + j
                        wbr = wload(wbd[l, d * 2 + 0].rearrange("p (j m) -> p j m", j=32)[:, j:j + 1, :], 1, 128)
                        wbi = wload(wbd[l, d * 2 + 1].rearrange("p (j m) -> p j m", j=32)[:, j:j + 1, :], 1, 128)
                        for h2 in range(2):
                            mm(ps(2 + h2), wbr[:, 0, :], USB[:, h2 * 512:(h2 + 1) * 512])
                            mm(ps(4 + h2), wbi[:, 0, :], USB[:, h2 * 512:(h2 + 1) * 512])
                        RR = PSt[:, 2 * 512: 4 * 512]
                        RI = PSt[:, 4 * 512: 6 * 512]
                        U, Kt, FA, SN, CS = a32(2, TL), AI32[:, 3 * 1024: 3 * 1024 + TL], a32(4, TL), a32(5, TL), a32(6, TL)
                        ts(U, IOTA[:, 0:TL], S(4)[:, col:col + 1], None, ALU.mult)
                        cp(Kt, U)
                        tt(U, U, Kt, ALU.subtract)
                        stt(FA, U, -1.0, U, ALU.mult, ALU.max)
                        act(SN, U, AF.Sin, scale=TWO_PI)
                        act(CS, FA, AF.Sin, scale=-TWO_PI, bias=math.pi / 2)
                        GR, GI, T1, T2 = a32(7), a32(8), a32(9), a32(10)
                        HRb, HIb = a16(11), a16(11, 1024, 1024)
                        cs_b = CS.unsqueeze(1).to_broadcast([128, nseq, L])
                        sn_b = SN.unsqueeze(1).to_broadcast([128, nseq, L])
                        rr, ri = sv(RR), sv(RI)
                        if d == 1:
                            rr, ri = rr[:, :, ::-1], ri[:, :, ::-1]
                        tt(sv(T1), rr, cs_b, ALU.mult)
                        tt(sv(T2), ri, sn_b, ALU.mult)
                        tt(GR, T1, T2, ALU.add)
                        tt(sv(T1), ri, cs_b, ALU.mult)
                        tt(sv(T2), rr, sn_b, ALU.mult)
                        tt(GI, T1, T2, ALU.subtract)
                        rho = S(3)[:, col:col + 1]
                        for s in range(nseq):
                            sl = slice(s * L, (s + 1) * L)
                            ir = H0t[:, 0, col:col + 1] if isS else 0.0
                            ii = H0t[:, 1, col:col + 1] if isS else 0.0
                            scan(GR[:, sl], rho.to_broadcast([128, L]), GR[:, sl], ir)
                            scan(GI[:, sl], rho.to_broadcast([128, L]), GI[:, sl], ii)
                        if not isS:
                            gl_r, gl_i = sv(GR)[:, :, L - 1], sv(GI)[:, :, L - 1]
                            c1 = CS[:, L - 1:L].to_broadcast([128, 4])
                            s1 = SN[:, L - 1:L].to_broadcast([128, 4])
                            f1, f2, f3, f4 = SMt[:, 0:4], SMt[:, 4:8], SMt[:, 8:12], SMt[:, 12:16]
                            tt(f1, gl_r, c1, ALU.mult)
                            tt(f2, gl_i, s1, ALU.mult)
                            tt(f1, f1, f2, ALU.subtract)
                            tt(f2, gl_i, c1, ALU.mult)
                            tt(f3, gl_r, s1, ALU.mult)
                            tt(f2, f2, f3, ALU.add)
                            qrc = S(11)[:, col:col + 1]
                            qic = S(12)[:, col:col + 1]
                            FS = FSt[:, :, :].rearrange("p r (s d j) -> p r s d j", s=4, d=2)
                            ts(f3, f2, qic, None, ALU.mult)
                            stt(FS[:, 0, :, d, j], f1, qrc, f3, ALU.mult, ALU.subtract)
                            ts(f4, f1, qic, None, ALU.mult)
                            stt(FS[:, 1, :, d, j], f2, qrc, f4, ALU.mult, ALU.add)
                        hr_o, hi_o = sv(HRb), sv(HIb)
                        if d == 1:
                            hr_o, hi_o = hr_o[:, :, ::-1], hi_o[:, :, ::-1]
                        tt(sv(T1), sv(GR), cs_b, ALU.mult)
                        tt(sv(T2), sv(GI), sn_b, ALU.mult)
                        tt(hr_o, sv(T1), sv(T2), ALU.subtract)
                        tt(sv(T1), sv(GI), cs_b, ALU.mult)
                        tt(sv(T2), sv(GR), sn_b, ALU.mult)
                        tt(hi_o, sv(T1), sv(T2), ALU.add)
                        last = (qq == 3 and d == 1)
                        for h2 in range(2):
                            mm(ps(6 + h2), CWP[:, d, 0, j, :], HRb[:, h2 * 512:(h2 + 1) * 512], start=first, stop=False)
                            mm(ps(6 + h2), CWP[:, d, 1, j, :], HIb[:, h2 * 512:(h2 + 1) * 512], start=False, stop=last)
                        first = False
                for h2 in range(2):
                    tmp = a32(9, 512, h2 * 512)
                    stt(tmp, US[:, h2 * 512:(h2 + 1) * 512], PV(PV_S5D + c), ps(6 + h2), ALU.mult, ALU.add)
                    act(SSMP[:, c, h2 * 512:(h2 + 1) * 512], tmp, AF.Gelu_apprx_tanh)

            for c in range(8):
                proj(win[l], 16, 3584 + c * 128, 128, Hr, us_cons, banks=(0, 1), n0=c)
            if not isS:
                for r in range(2):
                    for half in range(2):
                        mm(ps(7, 128), FSt[:, r, half * 128:(half + 1) * 128], ID)
                        o = SMt[:, 128 + half * 128: 256 + half * 128]
                        act(o, ps(7, 128), AF.Copy)
                        dma("sp", ns5[r][l, half * 128:(half + 1) * 128, :], o)

            def glu_cons(n, hf, b):
                g = a32(9, 512, hf * 512)
                act(g, ps(b), AF.Sigmoid, bias=PV(PV_GLUB + n))
                tt(E16[:, 16 + n, hf * 512:(hf + 1) * 512], g, SSMP[:, n, hf * 512:(hf + 1) * 512], ALU.mult)

            proj(glw[l], 8, 0, 1024, lambda k, hf: SSMP[:, k, hf * 512:(hf + 1) * 512], glu_cons, banks=(0, 1, 2, 3))

            if stop_at == (l, 5):
                raise _Stop()
            wps_l = (wpa[l], wpr[l], wps[l])
            for n in range(16):
                Gs = [[a32(i, 512, hf * 512) for hf in range(2)] for i in range(3)]
                for i in range(3):
                    def g_cons(n_, hf, b, i=i):
                        act(Gs[i][hf], ps(b), AF.Sigmoid)
                    proj(win[l], 16, 4608 + i * 2048 + n * 128, 128, Hr, g_cons, banks=(0, 1), n0=n)
                for i in range(3):
                    def p_cons(n_, hf, b, i=i):
                        macc = a32(3, 512, hf * 512)
                        tmp = a32(4, 512, hf * 512)
                        if i == 0:
                            tt(macc, Gs[0][hf], ps(b), ALU.mult)
                        elif i == 1:
                            tt(tmp, Gs[1][hf], ps(b), ALU.mult)
                            tt(macc, macc, tmp, ALU.add)
                        else:
                            tt(tmp, Gs[2][hf], ps(b), ALU.mult)
                            tt(E16[:, 24 + n, hf * 512:(hf + 1) * 512], macc, tmp, ALU.add)
                    proj(wps_l[i], 8, n * 128, 128, lambda k, hf, i=i: E16[:, i * 8 + k, hf * 512:(hf + 1) * 512], p_cons,
                         banks=(2, 3), n0=n)

            if stop_at == (l, 6):
                raise _Stop()
            for c in range(16):
                dma("sp", XA(c), (xin[ps_name] if (l == 0 or lmap == "noxsp") else xsp)[:, c, :])

            def o_cons(n, hf, b):
                stt(XA(n, hf), ps(b), MOD(2, n), XA(n, hf), ALU.mult, ALU.add)

            proj(wo[l], 16, 0, D, lambda k, hf: E16[:, 24 + k, hf * 512:(hf + 1) * 512], o_cons)

            if stop_at == (l, 7):
                raise _Stop()
            norm_to_H(1, 3)

            if stop_at == (l, 8):
                raise _Stop()
            def a_cons(n, hf, b):
                f = n
                Af = e32(32)
                act(Af[:, hf * 512:(hf + 1) * 512], ps(b), AF.Copy)
                if hf == 0:
                    return
                Y = e32(34)
                fcw = lambda jj: PV(PV_FCW + jj * 32 + f)
                act(Y, Af, AF.Identity, scale=fcw(1), bias=PV(PV_FCB + f))
                Xs, Ys = sv(Af), sv(Y)
                stt(Ys[:, :, 1:L], Xs[:, :, 0:L - 1], fcw(0), Ys[:, :, 1:L], ALU.mult, ALU.add)
                stt(Ys[:, :, 0:L - 1], Xs[:, :, 1:L], fcw(2), Ys[:, :, 0:L - 1], ALU.mult, ALU.add)
                act(e32(36), Y, AF.Gelu_apprx_tanh)

            def b_cons(n, hf, b):
                f = n
                tt(E16[:, f, hf * 512:(hf + 1) * 512], e32(36)[:, hf * 512:(hf + 1) * 512], ps(b), ALU.mult)

            for f in range(32):
                proj(wup[l], 16, f * 128, 128, Hr, a_cons, banks=(0, 1), n0=f)
                proj(wup[l], 16, 4096 + f * 128, 128, Hr, b_cons, banks=(2, 3), n0=f)

            if stop_at == (l, 9):
                raise _Stop()
            lastl = (l == depth - 1)

            def d_cons(n, hf, b):
                stt(XA(n, hf), ps(b), MOD(5, n), XA(n, hf), ALU.mult, ALU.add)
                if hf == 1:
                    dma("sp", (yout[ps_name] if (lastl or lmap == "noxsp") else xsp)[:, n, :], XA(n))

            proj(wdn[l], 32, 0, D, lambda k, hf: E16[:, k, hf * 512:(hf + 1) * 512], d_cons)

    for pn in passes:
        try:
            run_pass(pn)
        except _Stop:
            pass

    semnames = {}
    from contextlib import ExitStack
    with ExitStack() as es:
        def getsem(sk):
            if sk not in semnames:
                semnames[sk] = es.enter_context(nc.semaphore("s_%s_%s" % sk))
            return semnames[sk]
        for e in P.q:
            getsem((e, "c"))
        for e, n in NDMA.items():
            for i in range(n):
                getsem((e, i))
        block = es.enter_context(nc.Block())

        needed = {}
        for e in P.q:
            for fn, waits, tok in P.q[e]:
                for sk, v in waits:
                    if sk[1] == "c":
                        needed.setdefault(sk, set()).add(v)
        valmap = {}
        for sk, st_ in needed.items():
            valmap[sk] = {idx: i + 1 for i, idx in enumerate(sorted(st_))}

        def emit(engname):
            def f(eng):
                for fn, waits, tok in P.q[engname]:
                    for sk, v in waits:
                        eng.wait_ge(getsem(sk), valmap[sk][v] if sk[1] == "c" else v)
                    ins = fn(eng)
                    if tok[0][1] != "c":
                        ins.then_inc(getsem(tok[0]), 16)
                    elif tok[1] in needed.get(tok[0], ()):
                        ins.then_inc(getsem(tok[0]), 1)
                if engname in NDMA:
                    n = NDMA[engname]
                    tot = P.dman[engname]
                    for i in range(n):
                        cntd = len(range(i, tot, n))
                        if cntd:
                            eng.wait_ge(getsem((engname, i)), 16 * cntd)
            return f

        block.tensor(emit("pe"))
        block.scalar(emit("act"))
        block.vector(emit("dve"))
        block.gpsimd(emit("pool"))
        block.sync(emit("sp"))
    return nc


def _fm(x2d):
    t, d = x2d.shape
    return np.ascontiguousarray(x2d.reshape(t, d // 128, 128).transpose(2, 1, 0))


def _vec_fm(v):
    v = np.asarray(v)
    lead = v.shape[:-1]
    n = v.shape[-1] // 128
    r = v.reshape(*lead, n, 128)
    r = np.moveaxis(r, -1, 0)
    return np.ascontiguousarray(r.reshape(128, -1))


def _prep_shared(inp):
    f = np.float32
    sh = {}
    for k_src, k_dst in (("w_mod", "wmod"), ("w_in", "win"), ("w_pa", "wpa"), ("w_pr", "wpr"), ("w_ps", "wps"),
                         ("w_o", "wo"), ("w_up", "wup"), ("w_down", "wdn"), ("glu_w", "glw")):
        sh[k_dst] = np.ascontiguousarray(inp[k_src], dtype=f)
    sh["lgw"] = np.ascontiguousarray(np.asarray(inp["lru_gate_w"], f).transpose(0, 4, 1, 2, 3, 5).reshape(DEPTH, 128, 32 * 128))
    pv = np.zeros((128, DEPTH, NPV), f)
    for l in range(DEPTH):
        pv[:, l, PV_GMIX:PV_GMIX + 16] = _vec_fm(inp["g_mix"][l])
        pv[:, l, PV_GFFN:PV_GFFN + 16] = _vec_fm(inp["g_ffn"][l])
        pv[:, l, PV_QG] = inp["q_gain"][l]
        pv[:, l, PV_KG] = inp["k_gain"][l]
        pv[:, l, PV_RCW:PV_RCW + 32] = _vec_fm(inp["rnn_conv_w"][l])
        pv[:, l, PV_RCB:PV_RCB + 8] = _vec_fm(inp["rnn_conv_b"][l])
        pv[:, l, PV_LGB:PV_LGB + 32] = _vec_fm(inp["lru_gate_b"][l])
        pv[:, l, PV_LAM:PV_LAM + 16] = _vec_fm(inp["lru_lambda"][l])
        pv[:, l, PV_S5D:PV_S5D + 8] = _vec_fm(inp["s5_d"][l])
        pv[:, l, PV_GLUB:PV_GLUB + 8] = _vec_fm(inp["glu_b"][l])
        pv[:, l, PV_FCW:PV_FCW + 96] = _vec_fm(inp["ffn_conv_w"][l])
        pv[:, l, PV_FCB:PV_FCB + 32] = _vec_fm(inp["ffn_conv_b"][l])
        pv[:, l, PV_BMOD:PV_BMOD + 96] = _vec_fm(inp["b_mod"][l])
    sh["pvec"] = pv

    def scanlay(a):
        a = np.asarray(a, f)
        lead = a.shape[:-2]
        r = a.reshape(*lead, 32, 2, 64)
        r = np.moveaxis(r, -3, -1)
        return r.reshape(*lead, 128, 32)

    s5 = np.zeros((DEPTH, 128, 3, 64), f)
    are = scanlay(inp["s5_a_re"])
    aim = scanlay(inp["s5_a_im"])
    ldt = scanlay(np.broadcast_to(np.asarray(inp["s5_log_dt"], f)[..., None], (DEPTH, 2, 64, 64)))
    for d in range(2):
        s5[:, :, 0, d * 32:(d + 1) * 32] = are[:, d]
        s5[:, :, 1, d * 32:(d + 1) * 32] = aim[:, d]
        s5[:, :, 2, d * 32:(d + 1) * 32] = ldt[:, d]
    sh["s5sc"] = s5
    wb = np.zeros((DEPTH, 4, 128, 32, 128), f)
    cw = np.zeros((DEPTH, 128, 4, 32, 32), f)
    for ri, (bk, ck) in enumerate((("s5_b_re", "s5_c_re"), ("s5_b_im", "s5_c_im"))):
        b = np.asarray(inp[bk], f)
        c = np.asarray(inp[ck], f)
        for j in range(32):
            for g2 in range(2):
                g = 2 * j + g2
                g8 = g % 8
                wb[:, ri::2][:, :, g8 * 16:(g8 + 1) * 16, j, g2 * 64:(g2 + 1) * 64] = b[:, :, g].transpose(0, 1, 3, 2)
                for d in range(2):
                    cw[:, g2 * 64:(g2 + 1) * 64, d * 2 + ri, j, g2 * 16:(g2 + 1) * 16] = c[:, d, g].transpose(0, 2, 1)
    sh["wbd"] = np.ascontiguousarray(wb.reshape(DEPTH, 4, 128, 4096))
    sh["cwc"] = np.ascontiguousarray(cw.reshape(DEPTH, 128, 4, 1024))
    t = np.arange(T)
    inv = 1.0 / (10000.0 ** (np.arange(0, 64, 2, dtype=np.float64) / 64.0))
    pos = [np.floor(t / 64), t % 64]
    cosd = np.zeros((128, T), f)
    sind = np.zeros((128, T), f)
    for dd in range(128):
        ax = dd // 64
        i = dd % 32
        ang = (pos[ax].astype(np.float32) * inv[i].astype(np.float32)).astype(np.float32)
        cosd[dd] = np.cos(ang)
        sind[dd] = np.sin(ang)
    prot = np.zeros((128, 128), f)
    for m in range(128):
        if (m % 64) < 32:
            prot[m + 32, m] = -1.0
        else:
            prot[m - 32, m] = 1.0
    sh["cosd"], sh["sind"], sh["protd"] = cosd, sind, prot
    sh["identd"] = np.eye(128, dtype=f)
    sh["iotad"] = np.ascontiguousarray(np.broadcast_to(np.arange(1, T + 1, dtype=f)[None, :], (128, T)))
    return sh, scanlay


def core_map(inp, sh, scanlay, core):
    f = np.float32
    b = core % 4
    m = dict(sh)
    m["xp"] = _fm(np.asarray(inp["x_prompt"][4 * core:4 * core + 4], f).reshape(T, D))
    m["xs"] = _fm(np.asarray(inp["x_sample"][b], f))
    cvv_ = np.zeros((128, 16, 2), f)
    cvv_[:, :, 0] = _vec_fm(inp["c_ctx"])
    cvv_[:, :, 1] = _vec_fm(inp["c"][b])
    m["cv"] = cvv_
    h0 = np.zeros((DEPTH, 128, 2, 64), f)
    sr = scanlay(inp["state_s5_re"][b])
    si = scanlay(inp["state_s5_im"][b])
    for d in range(2):
        h0[:, :, 0, d * 32:(d + 1) * 32] = sr[:, d]
        h0[:, :, 1, d * 32:(d + 1) * 32] = si[:, d]
    m["h0s5"] = h0
    hl = np.zeros((DEPTH, 128, 16), f)
    for l in range(DEPTH):
        hl[l] = _vec_fm(inp["state_lru"][b, l])
    m["h0lru"] = hl
    ck = np.asarray(inp["cache_k"][b], f)
    m["ckT"] = np.ascontiguousarray(ck.transpose(0, 3, 2, 1))
    cvx = np.asarray(inp["cache_v"][b], f).reshape(DEPTH, 2, 128, 256)
    m["cvv"] = np.ascontiguousarray(cvx.transpose(0, 2, 1, 3))
    return m


def kernel(**inp):
    f = np.float32
    sh, scanlay = _prep_shared(inp)
    nc = build_program()
    in_maps = [core_map(inp, sh, scanlay, core) for core in range(8)]
    res = run_bass_kernel_spmd(nc, in_maps, core_ids=list(range(8)))
    R = res.results

    def unfm(a):
        return np.ascontiguousarray(a.transpose(2, 1, 0).reshape(T, D))

    y_p = np.stack([unfm(R[c]["yp"]).reshape(4, 256, D) for c in range(8)]).reshape(32, 256, D)
    y_s = np.stack([unfm(R[c]["ys"]) for c in range(4)])
    nk_ = np.zeros((32, DEPTH, 256, 2, 128), f)
    nv_ = np.zeros((32, DEPTH, 256, 2, 128), f)
    nl_ = np.zeros((32, DEPTH, 2, 1024), f)
    nr_ = np.zeros((32, DEPTH, 2, 64, 64), f)
    ni_ = np.zeros((32, DEPTH, 2, 64, 64), f)
    for c in range(8):
        k = R[c]["nk"].reshape(DEPTH, 4, 256, 2, 128)
        v = R[c]["nv"].reshape(DEPTH, 4, 256, 2, 128)
        nk_[4 * c:4 * c + 4] = k.transpose(1, 0, 2, 3, 4)
        nv_[4 * c:4 * c + 4] = v.transpose(1, 0, 2, 3, 4)
        lr = R[c]["nlru"].reshape(DEPTH, 4, 2, 8 * 128)
        nl_[4 * c:4 * c + 4] = lr.transpose(1, 0, 2, 3)
        for dst, key in ((nr_, "ns5r"), (ni_, "ns5i")):
            s = R[c][key].reshape(DEPTH, 4, 2, 32, 2, 64)
            dst[4 * c:4 * c + 4] = s.transpose(1, 0, 2, 3, 4, 5).reshape(4, DEPTH, 2, 64, 64)
    return (y_p.astype(f), y_s.astype(f), nk_, nv_, nl_, nr_, ni_)
```
